# Optimizing a Trainium2 kernel written in Bass

```python
import math
import jax
import jax.numpy as jnp
from jax import lax
import numpy as np

D_MODEL = 1024
BATCH = 4
SEQ = 4096
DEPTH = 4

GRID_W = 64
CTX_LEN = 256
N_MOD = 9
MACARON = 0.5
ALPHA = (2 * DEPTH) ** 0.25
BETA = (8 * DEPTH) ** -0.25
LN_EPS = 1e-5
RMS_EPS = 1e-6
F_TINY = 1e-20
FFN_HIDDEN = 2816

HG_DK = 128
HG_DV = 128
HG_HEADS = (D_MODEL // 2) // HG_DV
HG_K = HG_HEADS * HG_DK
HG_V = HG_HEADS * HG_DV
HG_CHUNK = 64
POOL_WINDOWS = (2, 4, 8, 16)
POOL_GROUPS = 4
POOL_WIDTH = D_MODEL // 2
POOL_GC = POOL_WIDTH // POOL_GROUPS
EVEN_SIZES = (HG_K, HG_V, HG_V, HG_K, HG_K, POOL_WIDTH)
EVEN_IN = sum(EVEN_SIZES)
EVEN_SPLITS = tuple(sum(EVEN_SIZES[:i + 1]) for i in range(len(EVEN_SIZES) - 1))
EVEN_MIX = HG_V + POOL_WIDTH

DA_HD = 64
DA_VD = 2 * DA_HD
DA_HEADS = D_MODEL // DA_VD
DA_QK = DA_HEADS * 2 * DA_HD
ODD_IN = 2 * DA_QK + DA_HEADS * DA_VD
ODD_MIX = DA_HEADS * DA_VD
Q_BLOCK = 128
ROPE_BASE = 10000.0
ROPE_AXIS = DA_HD // 2

N_EVEN = (DEPTH + 1) // 2
N_ODD = DEPTH // 2

kernel_name = 'hybrid_hgrn2_pool_diffattn_macaron'


def layer_norm(x, g, b):
    xf = x.astype(jnp.float32)
    mu = jnp.mean(xf, axis=-1, keepdims=True)
    xc = xf - mu
    var = jnp.mean(xc * xc, axis=-1, keepdims=True)
    y = xc * lax.rsqrt(var + LN_EPS) * g.astype(jnp.float32) + b.astype(jnp.float32)
    return y.astype(x.dtype)


def rms_norm(x, w):
    xf = x.astype(jnp.float32)
    y = xf * lax.rsqrt(jnp.mean(xf * xf, axis=-1, keepdims=True) + RMS_EPS)
    return (y * w.astype(jnp.float32)).astype(x.dtype)


def modulate(x, shift, scale):
    return x * (1.0 + scale) + shift


def swiglu(h, w_in, w_out):
    a, b = jnp.split(h @ w_in, 2, axis=-1)
    return (jax.nn.silu(a) * b) @ w_out


def ffn_substep(x, mods, w_in, w_out, g, b):
    shift, scale, gate = mods
    y = swiglu(modulate(x, shift, scale), w_in, w_out)
    return layer_norm(ALPHA * x + MACARON * gate * y, g, b)


def to_heads(a, hd):
    bsz, n, _ = a.shape
    return a.reshape(bsz, n, -1, hd).transpose(0, 2, 1, 3)


def from_heads(a):
    bsz, nh, n, hd = a.shape
    return a.transpose(0, 2, 1, 3).reshape(bsz, n, nh * hd)


def gla_chunk_scan(q, k, v, log_f, s0):
    bsz, nh, t, _ = q.shape
    dv = v.shape[-1]
    n = t // HG_CHUNK

    def to_chunks(a):
        return jnp.moveaxis(a.reshape(bsz, nh, n, HG_CHUNK, a.shape[-1]), 2, 0)

    causal = jnp.tril(jnp.ones((HG_CHUNK, HG_CHUNK), dtype=bool))

    def step(s, inp):
        qi, ki, vi, gi = inp
        b = jnp.cumsum(gi.astype(jnp.float32), axis=-2)
        diff = b[..., :, None, :] - b[..., None, :, :]
        decay = jnp.where(causal[:, :, None], jnp.exp(jnp.minimum(diff, 0.0)), 0.0)
        scores = jnp.einsum('bhtd,bhsd,bhtsd->bhts', qi, ki, decay)
        o = (jnp.einsum('bhts,bhsv->bhtv', scores, vi)
             + jnp.einsum('bhtd,bhdv->bhtv', qi * jnp.exp(b), s))
        b_last = b[..., -1:, :]
        s_new = (jnp.exp(b_last[..., 0, :])[..., None] * s
                 + jnp.einsum('bhsd,bhsv->bhdv', ki * jnp.exp(b_last - b), vi))
        return s_new, o

    s_fin, o = lax.scan(step, s0, (to_chunks(q), to_chunks(k), to_chunks(v), to_chunks(log_f)))
    o = jnp.moveaxis(o, 0, 2).reshape(bsz, nh, t, dv)
    return o, s_fin


def hgrn2_gates(f_raw, lb):
    z = to_heads(f_raw, HG_DK).astype(jnp.float32)
    lbh = lb.reshape(HG_HEADS, 1, HG_DK)
    f = lbh + (1.0 - lbh) * jax.nn.sigmoid(z)
    log_f = jnp.log(jnp.maximum(f, F_TINY))
    return log_f, 1.0 - f


def hgrn2_bidir(q_c, i_c, f_c, q_l, i_l, f_l, lb):
    flip = lambda a: jnp.flip(a, axis=2)
    bsz = q_c.shape[0]
    zero = jnp.zeros((bsz, HG_HEADS, HG_DK, HG_DV), jnp.float32)
    lf, k = hgrn2_gates(f_c[0], lb[0])
    o_cf, s_cf = gla_chunk_scan(q_c, k, i_c, lf, zero)
    lf, k = hgrn2_gates(f_l[0], lb[0])
    o_lf, _ = gla_chunk_scan(q_l, k, i_l, lf, s_cf)
    lf, k = hgrn2_gates(f_c[1], lb[1])
    o_cb, s_cb = gla_chunk_scan(flip(q_c), flip(k), flip(i_c), flip(lf), zero)
    lf, k = hgrn2_gates(f_l[1], lb[1])
    o_lb, _ = gla_chunk_scan(flip(q_l), flip(k), flip(i_l), flip(lf), s_cb)
    return o_cf + flip(o_cb), o_lf + flip(o_lb)


def multiscale_pool(u, pool_w, pool_scale):
    bsz, n, _ = u.shape
    uf = u.astype(jnp.float32)
    csum = jnp.concatenate([jnp.zeros((bsz, 1, POOL_WIDTH), jnp.float32), jnp.cumsum(uf, axis=1)], axis=1)
    pos = jnp.arange(n)
    groups = []
    for gi, w in enumerate(POOL_WINDOWS):
        lo = jnp.clip(pos - w // 2, 0, n)
        hi = jnp.clip(pos + (w - w // 2), 0, n)
        sl = slice(gi * POOL_GC, (gi + 1) * POOL_GC)
        win_sum = csum[:, hi, sl] - csum[:, lo, sl]
        count = (hi - lo).astype(jnp.float32)[None, :, None]
        groups.append(win_sum / count - uf[:, :, sl])
    pooled = jnp.stack(groups, axis=2)
    y = jnp.einsum('bngc,gcd->bngd', pooled, pool_w.astype(jnp.float32)).reshape(bsz, n, POOL_WIDTH)
    return (y * pool_scale.astype(jnp.float32)).astype(u.dtype)


def hgrn2_pool_mixer(h_lat, h_ctx, w_in, w_out, lb, norm_w, pool_w, pool_scale, need_ctx):
    def split(h):
        q, i, g, f_fw, f_bw, u = jnp.split(h @ w_in, EVEN_SPLITS, axis=-1)
        return to_heads(q, HG_DK) * HG_DK ** -0.5, to_heads(i, HG_DV), g, (f_fw, f_bw), u

    q_l, i_l, g_l, f_l, u_l = split(h_lat)
    q_c, i_c, g_c, f_c, u_c = split(h_ctx)
    o_c, o_l = hgrn2_bidir(q_c, i_c, f_c, q_l, i_l, f_l, lb)

    def readout(o, g, u):
        rec = from_heads(rms_norm(o, norm_w)).astype(u.dtype) * jax.nn.silu(g)
        return jnp.concatenate([rec, multiscale_pool(u, pool_w, pool_scale)], axis=-1) @ w_out

    y_lat = readout(o_l, g_l, u_l)
    y_ctx = readout(o_c, g_c, u_c) if need_ctx else None
    return y_lat, y_ctx


def axial_rope_tables(t):
    rows = t // GRID_W
    row = jnp.repeat(jnp.arange(rows, dtype=jnp.float32), GRID_W)
    col = jnp.tile(jnp.arange(GRID_W, dtype=jnp.float32), rows)
    inv_freq = ROPE_BASE ** (-jnp.arange(0, ROPE_AXIS, 2, dtype=jnp.float32) / ROPE_AXIS)
    ang_r = row[:, None] * inv_freq
    ang_c = col[:, None] * inv_freq
    ang = jnp.concatenate([ang_r, ang_r, ang_c, ang_c], axis=-1)
    return jnp.cos(ang), jnp.sin(ang)


def rotate_half(a):
    a1, a2 = jnp.split(a, 2, axis=-1)
    return jnp.concatenate([-a2, a1], axis=-1)


def apply_axial_rope(a, cos, sin):
    a_r, a_c = jnp.split(a, 2, axis=-1)
    return a * cos + jnp.concatenate([rotate_half(a_r), rotate_half(a_c)], axis=-1) * sin


def lambda_init(layer):
    return 0.8 - 0.6 * math.exp(-0.3 * layer)


def diff_attention_mixer(h_lat, h_ctx, w_in, w_out, lam_vec, sub_w, lam_init, need_ctx):
    bsz, t, _ = h_lat.shape

    def project(h):
        n = h.shape[1]
        q, k, v = jnp.split(h @ w_in, (DA_QK, 2 * DA_QK), axis=-1)
        q = q.reshape(bsz, n, DA_HEADS, 2, DA_HD).transpose(0, 2, 3, 1, 4) * DA_HD ** -0.5
        k = k.reshape(bsz, n, DA_HEADS, 2, DA_HD).transpose(0, 2, 3, 1, 4)
        return q, k, to_heads(v, DA_VD)

    q_l, k_l, v_l = project(h_lat)
    q_c, k_c, v_c = project(h_ctx)
    cos, sin = axial_rope_tables(t)
    q_l = apply_axial_rope(q_l, cos, sin).astype(h_lat.dtype)
    k_l = apply_axial_rope(k_l, cos, sin).astype(h_lat.dtype)
    lv = lam_vec.astype(jnp.float32)
    lam = jnp.exp(jnp.sum(lv[0] * lv[1])) - jnp.exp(jnp.sum(lv[2] * lv[3])) + lam_init

    def diff_attend(q, k, v):
        s = jnp.einsum('bhmqd,bhmkd->bhmqk', q, k).astype(jnp.float32)
        p = jax.nn.softmax(s, axis=-1)
        w = p[:, :, 0] - lam * p[:, :, 1]
        return jnp.einsum('bhqk,bhkv->bhqv', w.astype(v.dtype), v)

    k_all = jnp.concatenate([k_c, k_l], axis=3)
    v_all = jnp.concatenate([v_c, v_l], axis=2)
    n_blk = t // Q_BLOCK
    q_blocks = jnp.moveaxis(q_l.reshape(bsz, DA_HEADS, 2, n_blk, Q_BLOCK, DA_HD), 3, 0)
    o_l = lax.map(lambda qb: diff_attend(qb, k_all, v_all), q_blocks)
    o_l = jnp.moveaxis(o_l, 0, 2).reshape(bsz, DA_HEADS, t, DA_VD)

    def readout(o):
        return from_heads(rms_norm(o, sub_w) * (1.0 - lam_init)) @ w_out

    y_lat = readout(o_l)
    y_ctx = readout(diff_attend(q_c, k_c, v_c)) if need_ctx else None
    return y_lat, y_ctx


def setup_inputs(seed: int = 0) -> dict:
    key = jax.random.key(seed)
    ks = jax.random.split(key, 20)
    D = D_MODEL

    def nrm(k, shape, s):
        return jax.random.normal(k, shape, jnp.float32) * s

    return {
        'x': nrm(ks[0], (BATCH, SEQ, D), 1.0),
        'c': nrm(ks[1], (BATCH, D), 1.0),
        'ctx': nrm(ks[2], (BATCH, CTX_LEN, D), 1.0),
        'c_ctx': nrm(ks[3], (D,), 1.0),
        'w_ada': nrm(ks[4], (DEPTH, D, N_MOD * D), 0.5 * D ** -0.5),
        'b_ada': nrm(ks[5], (DEPTH, N_MOD * D), 0.02),
        'ln_g': 1.0 + nrm(ks[6], (DEPTH, 3, D), 0.02),
        'ln_b': nrm(ks[7], (DEPTH, 3, D), 0.02),
        'w_ffn_in': nrm(ks[8], (DEPTH, 2, D, 2 * FFN_HIDDEN), D ** -0.5),
        'w_ffn_out': nrm(ks[9], (DEPTH, 2, FFN_HIDDEN, D), BETA * FFN_HIDDEN ** -0.5),
        'w_in_even': nrm(ks[10], (N_EVEN, D, EVEN_IN), D ** -0.5),
        'w_out_even': nrm(ks[11], (N_EVEN, EVEN_MIX, D), BETA * EVEN_MIX ** -0.5),
        'hg_lb': nrm(ks[12], (N_EVEN, 2, HG_K), 0.5),
        'hg_norm_w': 1.0 + nrm(ks[13], (N_EVEN, HG_DV), 0.02),
        'pool_w': nrm(ks[14], (N_EVEN, POOL_GROUPS, POOL_GC, POOL_GC), POOL_GC ** -0.5),
        'pool_scale': 1.0 + nrm(ks[15], (N_EVEN, POOL_WIDTH), 0.1),
        'w_in_odd': nrm(ks[16], (N_ODD, D, ODD_IN), D ** -0.5),
        'w_out_odd': nrm(ks[17], (N_ODD, ODD_MIX, D), BETA * ODD_MIX ** -0.5),
        'da_lambda': nrm(ks[18], (N_ODD, 4, DA_HD), 0.1),
        'da_sub_w': 1.0 + nrm(ks[19], (N_ODD, DA_VD), 0.02),
    }


def reference(x, c, ctx, c_ctx, w_ada, b_ada, ln_g, ln_b, w_ffn_in, w_ffn_out,
              w_in_even, w_out_even, hg_lb, hg_norm_w, pool_w, pool_scale,
              w_in_odd, w_out_odd, da_lambda, da_sub_w):
    lb_soft = jax.nn.softmax(hg_lb.astype(jnp.float32), axis=0)
    lb_all = jnp.cumsum(lb_soft, axis=0) - lb_soft[0]
    c_act = jax.nn.silu(c)
    cc_act = jax.nn.silu(c_ctx)
    for layer in range(DEPTH):
        last = layer == DEPTH - 1
        m_lat = jnp.split((c_act @ w_ada[layer] + b_ada[layer])[:, None, :], N_MOD, axis=-1)
        m_ctx = jnp.split(cc_act @ w_ada[layer] + b_ada[layer], N_MOD, axis=-1)
        x = ffn_substep(x, m_lat[0:3], w_ffn_in[layer, 0], w_ffn_out[layer, 0], ln_g[layer, 0], ln_b[layer, 0])
        ctx = ffn_substep(ctx, m_ctx[0:3], w_ffn_in[layer, 0], w_ffn_out[layer, 0], ln_g[layer, 0], ln_b[layer, 0])
        h_lat = modulate(x, m_lat[3], m_lat[4])
        h_ctx = modulate(ctx, m_ctx[3], m_ctx[4])
        if layer % 2 == 0:
            e = layer // 2
            y_lat, y_ctx = hgrn2_pool_mixer(h_lat, h_ctx, w_in_even[e], w_out_even[e], lb_all[e],
                                            hg_norm_w[e], pool_w[e], pool_scale[e], not last)
        else:
            o = layer // 2
            y_lat, y_ctx = diff_attention_mixer(h_lat, h_ctx, w_in_odd[o], w_out_odd[o], da_lambda[o],
                                                da_sub_w[o], lambda_init(layer), not last)
        x = layer_norm(ALPHA * x + m_lat[5] * y_lat, ln_g[layer, 1], ln_b[layer, 1])
        x = ffn_substep(x, m_lat[6:9], w_ffn_in[layer, 1], w_ffn_out[layer, 1], ln_g[layer, 2], ln_b[layer, 2])
        if not last:
            ctx = layer_norm(ALPHA * ctx + m_ctx[5] * y_ctx, ln_g[layer, 1], ln_b[layer, 1])
            ctx = ffn_substep(ctx, m_ctx[6:9], w_ffn_in[layer, 1], w_ffn_out[layer, 1], ln_g[layer, 2], ln_b[layer, 2])
    return x
```

```python
import math
from contextlib import ExitStack

import numpy as np
import concourse.bass as bass
import concourse.mybir as mybir
from concourse.bass_utils import run_bass_kernel_spmd

F32 = mybir.dt.float32
BF16 = mybir.dt.bfloat16
AF = mybir.ActivationFunctionType
ALU = mybir.AluOpType

D = 1024
NCH = 8
TL = 2048
TC = 256
TT = TL + TC
DEPTH = 4
FH = 2816
NF = FH // 128
ALPHA = (2 * DEPTH) ** 0.25
LN_EPS = 1e-5
RMS_EPS = 1e-6
F_TINY = 1e-20
N_CORES = 8

TILES = [(0, 512), (512, 512), (1024, 512), (1536, 512), (2048, 256)]


class StopBuild(Exception):
    pass


class Reg:
    __slots__ = ("w", "r")

    def __init__(self):
        self.w = None
        self.r = {}


class Buf:
    def __init__(self, kb, name, shape, dtype, psum=False, view=None):
        if view is not None:
            self.t = view
        elif psum:
            self.t = kb.es.enter_context(kb.nc.psum_tensor(name, shape, dtype))
        else:
            self.t = kb.es.enter_context(kb.nc.sbuf_tensor(name, shape, dtype))
        self.regs = {}
        self.inherit = {}

    def reg(self, *key):
        r = self.regs.get(key)
        if r is None:
            r = self.regs[key] = Reg()
            r.r = dict(self.inherit)
        return r

    def allregs(self):
        return list(self.regs.values())


class KB:
    SEM_LIMIT = 12000
    NDS = 12

    def __init__(self, nc, es):
        self.nc, self.es = nc, es
        self.E = {"pe": nc.tensor, "act": nc.scalar, "dve": nc.vector, "pool": nc.gpsimd, "sp": nc.sync}
        self.esem, self.ecnt, self.nsem = {}, {}, 0
        for e in self.E:
            self._new_esem(e)
        self.seen = {e: {} for e in self.E}
        self.dsems = [es.enter_context(nc.semaphore(f"dq{i}")) for i in range(self.NDS)]
        self.dval = [0] * self.NDS
        self.dnext = 0
        self.out_events = []

    def _new_esem(self, e):
        self.nsem += 1
        self.esem[e] = self.es.enter_context(self.nc.semaphore(f"s{e}{self.nsem}"))
        self.ecnt[e] = 0

    def _wait(self, e, sem, val):
        if self.seen[e].get(sem, 0) >= val:
            return
        self.E[e].wait_ge(sem, val)
        self.seen[e][sem] = val

    @staticmethod
    def _deps(reads, writes):
        deps = {}

        def add(s, v):
            if deps.get(s, 0) < v:
                deps[s] = v

        for r in reads:
            if r.w is not None:
                add(*r.w)
        for w in writes:
            if w.w is not None:
                add(*w.w)
            for s, v in w.r.items():
                add(s, v)
        return deps

    @staticmethod
    def _update(ev, reads, writes):
        s, v = ev
        for r in reads:
            if r.r.get(s, 0) < v:
                r.r[s] = v
        for w in writes:
            w.w = ev
            w.r = {}

    def op(self, e, fn, reads=(), writes=()):
        own = self.esem[e]
        for s, v in self._deps(reads, writes).items():
            if e == "pe" and s is own:
                continue
            self._wait(e, s, v)
        fn(self.E[e]).then_inc(own, 1)
        self.ecnt[e] += 1
        ev = (own, self.ecnt[e])
        self._update(ev, reads, writes)
        if self.ecnt[e] >= self.SEM_LIMIT:
            self._new_esem(e)
        return ev

    def dma(self, q, out, in_, reads=(), writes=()):
        for s, v in self._deps(reads, writes).items():
            self._wait(q, s, v)
        i = self.dnext
        self.dnext = (i + 1) % self.NDS
        sem = self.dsems[i]
        if self.dval[i] > 0:
            self._wait(q, sem, self.dval[i])
        self.E[q].dma_start(out=out, in_=in_).then_inc(sem, 16)
        self.dval[i] += 16
        ev = (sem, self.dval[i])
        self._update(ev, reads, writes)
        return ev

    def collective(self, fn, reads=(), writes=()):
        if not hasattr(self, "ccreg"):
            self.ccreg = Reg()
            self.ccsem = self.es.enter_context(self.nc.semaphore("ccsem"))
            self.ccval = 0
        reads = list(reads) + [self.ccreg]
        writes = list(writes) + [self.ccreg]
        for s, v in self._deps(reads, writes).items():
            self._wait("pool", s, v)
        fn(self.E["pool"]).then_inc(self.ccsem, 1)
        self.ccval += 1
        ev = (self.ccsem, self.ccval)
        self._update(ev, reads, writes)
        return ev

    def alias(self, new_bufs, old_bufs):
        merged = {}

        def add(s, v):
            if merged.get(s, 0) < v:
                merged[s] = v

        for b in old_bufs:
            for s, v in b.inherit.items():
                add(s, v)
            for o in b.regs.values():
                if o.w is not None:
                    add(*o.w)
                for s, v in o.r.items():
                    add(s, v)
        for b in new_bufs:
            b.inherit = dict(merged)
            for n in b.regs.values():
                n.w = None
                n.r = dict(merged)


def wlayout(L):
    ent = [("ada", D, 9 * D), ("fin0", D, 2 * FH), ("fin1", D, 2 * FH), ("fout0", FH, D), ("fout1", FH, D),
           ("min", D, 3072), ("mout", D, D)]
    if L % 2 == 0:
        ent.append(("poolw", 512, 128))
    off, o = {}, 0
    for name, r, c in ent:
        off[name] = (o, r, c)
        o += r * c
    rows = -(-o // (8 * 2048))
    return off, rows


def build_program(stop="full", layers=tuple(range(DEPTH))):
    nc = bass.Bass("TRN2", target_bir_lowering=False)
    es = ExitStack()
    kb = KB(nc, es)
    n_layers = DEPTH

    def din(name, shape, dt=F32):
        return nc.dram_tensor(name, list(shape), dt, kind="ExternalInput").ap()

    xT_d = din("xT", [128, NCH, TL])
    ctxT_d = din("ctxT", [128, NCH, TC])
    cT_d = din("cT", [128, NCH, 2])
    b_adaT_d = din("b_adaT", [128, DEPTH, 72])
    lnT_d = din("lnT", [128, DEPTH, 3, 2, NCH])
    WOFF, WG = {}, {}
    gather_ev = {}
    for L in layers:
        off, rows = wlayout(L)
        WOFF[L] = off
        wsh = din(f"wsh{L}", [rows, 2048])
        bnc = nc.dram_tensor(f"wbn{L}", [rows, 2048], BF16)
        WG[L] = nc.dram_tensor(f"wg{L}", [8 * rows, 2048], BF16)
        r_b, r_g = Reg(), Reg()
        kb.dma("pool", bnc.ap(), wsh, writes=[r_b])
        gather_ev[L] = (r_b, r_g, bnc)

    def wv(L, name, k0, nk, f0, nf):
        o, r, c = WOFF[L][name]
        return bass.AP(WG[L], o + k0 * 128 * c + f0, [[c, 128], [128 * c, nk], [1, nf]])
    has_odd = any(L % 2 == 1 for L in layers)
    has_even = any(L % 2 == 0 for L in layers)
    if has_odd:
        rope_d = din("ropeT", [128, 2, TL])
        rotm_d = din("rotm", [128, 128])
        dalam_d = din("dalam", [128, 2, 256])
        subw_d = din("subwT", [128, 2])
        QD = nc.dram_tensor("qd", [128, 8 * TT], BF16)
        KS = nc.dram_tensor("ks", [8 * 128, TL], BF16)
        KC = nc.dram_tensor("kc", [128, 8 * TC], BF16)
        VS = nc.dram_tensor("vs", [8 * 128, 16 * 128], BF16)
        VC = nc.dram_tensor("vc", [128, 2 * D], BF16)
        KG = nc.dram_tensor("kg", [8 * 256, TL], BF16)
        VG = nc.dram_tensor("vg", [8 * 256, 16 * 128], BF16)
        DREG = Buf(kb, "", None, None, view=0)
    if has_even:
        hgc_d = din("hgc", [128, 768])
        cm2_d = din("cm2", [128, 512])
        idf_d = din("idf", [128, 128])
        band_d = din("band", [128, 6 * 4 * 128])
        hglb_d = din("hglbT", [128, 2, 2, 4])
        hgsm_d = din("hgsm", [128, 20])
        VD = nc.dram_tensor("vd", [128, 4 * 18 * 128], BF16)
        YPD = nc.dram_tensor("ypd", [128, 4 * TT], BF16)
        OAD = nc.dram_tensor("oad", [128, 4 * TT], F32)
        RECD = nc.dram_tensor("recd", [128, 4 * TT], BF16)
        XS = nc.dram_tensor("xs", [128, 512], F32)
        XG = nc.dram_tensor("xg", [256, 512], F32)
        HS = nc.dram_tensor("hs", [128, 512], BF16)
        HG = nc.dram_tensor("hg", [256, 512], BF16)
        EREG = Buf(kb, "", None, None, view=0)
    out_d = nc.dram_tensor("outT", [128, NCH, TL], F32, kind="ExternalOutput").ap()
    dbg_d = nc.dram_tensor("dbgT", [128, NCH, TC], F32, kind="ExternalOutput").ap()

    xT = Buf(kb, "xT_s", [128, NCH, TT], F32)
    hT = Buf(kb, "hT_s", [128, NCH, TT], BF16)
    WA = [Buf(kb, f"wa{i}", [128, NCH, 512], BF16) for i in range(2)]
    WB = [Buf(kb, f"wb{i}", [128, NCH, 512], BF16) for i in range(2)]
    SCB = es.enter_context(nc.sbuf_tensor("scb", [128, 15360], BF16))
    WO = [Buf(kb, "", None, None, view=SCB[:, 4096 * i:4096 * (i + 1)].rearrange("p (j d) -> p j d", j=4))
          for i in range(2)]
    GT = [Buf(kb, "", None, None, view=SCB[:, 8192 + 2048 * i:8192 + 2048 * (i + 1)].rearrange(
        "p (j d) -> p j d", j=4)) for i in range(2)]
    FFN_SCR = WO + GT
    KH = Buf(kb, "", None, None, view=SCB[:, 0:4352])
    VH = Buf(kb, "", None, None, view=SCB[:, 4352:8704].rearrange("p (k v) -> p k v", k=34))
    QH = [Buf(kb, "", None, None, view=SCB[:, 8704 + 2304 * i:8704 + 2304 * (i + 1)]) for i in range(2)]
    EB = [Buf(kb, "", None, None, view=SCB[:, 13312 + 512 * i:13312 + 512 * (i + 1)]) for i in range(4)]
    ATT_SCR = [KH, VH] + QH + EB
    def scv(a, b, pat=None, **kw):
        v = SCB[:, a:b]
        return Buf(kb, "", None, None, view=(v.rearrange(pat, **kw) if pat else v))
    VHD = scv(0, 2304, "p (k v) -> p k v", k=18)
    QT2 = scv(2304, 2816)
    KT2 = scv(2816, 3328)
    KM = scv(3328, 3840, "p (c d) -> p c d", c=4)
    AM = [scv(3840 + 128 * i, 3968 + 128 * i) for i in range(2)]
    SBF = scv(4096, 5120, "p (c d) -> p c d", c=8)
    HGCB = scv(5120, 5888)
    BANDB = scv(5888, 8960, "p (k g t) -> p k g t", k=6, g=4)
    STG = [scv(8960 + 512 * i, 9472 + 512 * i) for i in range(2)]
    UT = scv(9984, 11520, "p (j c) -> p j c", j=3)
    PL = scv(11520, 12032, "p (g t) -> p g t", g=4)
    UHB = scv(12032, 12544)
    PWB = scv(12544, 13056, "p (g d) -> p g d", g=4)
    XIN = scv(0, 4096, "p (k t) -> p k t", k=8)
    QM = scv(13056, 13568, "p (c t) -> p c t", c=4)
    CM2 = scv(13568, 14080, "p (c t) -> p c t", c=4)
    IDB = scv(14080, 14208)
    HG_SCR = [VHD, QT2, KT2, KM, SBF, HGCB, BANDB, UT, PL, UHB, PWB, XIN, QM, CM2, IDB] + AM + STG
    SIL = [Buf(kb, f"sil{i}", [128, 512], F32) for i in range(2)]
    SQ = Buf(kb, "sq", [128, 3, 512], F32)
    TMP = [Buf(kb, f"tmp{i}", [128, 512], F32) for i in range(2)]
    STAT = [Buf(kb, f"stat{i}", [128, 512], F32) for i in range(3)]
    MOD = Buf(kb, "mod", [128, DEPTH, NCH, 9, 2], F32)
    DER = Buf(kb, "der", [128, DEPTH, 10, NCH, 2], F32)
    LNP = Buf(kb, "lnp", [128, DEPTH, 3, 2, NCH], F32)
    BADA = Buf(kb, "bada", [128, DEPTH, 72], F32)
    CIN = Buf(kb, "cin", [128, NCH, 2], F32)
    CA = Buf(kb, "ca", [128, NCH, 2], BF16)
    ONES = Buf(kb, "ones32", [128, 128], F32)
    ONES128 = Buf(kb, "ones128", [128, 128], F32)
    ONESB = Buf(kb, "onesb", [128, 128], BF16)
    if has_even:
        IDF = Buf(kb, "idf_s", [128, 128], F32)
        MSK32 = Buf(kb, "msk32", [128, 256], F32)
        HGSM = Buf(kb, "hgsm_s", [128, 20], F32)
        HGLB = Buf(kb, "hglb_s", [128, 2, 2, 4], F32)
        LBB = Buf(kb, "lbb", [128, 4, 4], F32)
        SST = Buf(kb, "sst", [128, 4, 128], F32)
        SRX = Buf(kb, "srx", [128, 512], F32)
        STT_T = [Buf(kb, f"sttt{i}", [128, 128], F32) for i in range(2)]
    if has_odd:
        ROT = Buf(kb, "rot", [128, 128], F32)
        DAL = Buf(kb, "dal", [128, 2, 256], F32)
        SUBWT = Buf(kb, "subwt", [128, 2], F32)
        LSC = Buf(kb, "lsc", [128, 8], F32)
        LTMP = Buf(kb, "ltmp", [128, 64], F32)
    PS = [Buf(kb, f"ps{i}", [128, 512], F32, psum=True) for i in range(8)]

    for ti, (t0, n) in enumerate(TILES[:4]):
        kb.dma("sp", xT.t[:, :, t0:t0 + n], xT_d[:, :, t0:t0 + n],
               writes=[xT.reg(c, ti) for c in range(NCH)])
    kb.dma("sp", xT.t[:, :, TL:TT], ctxT_d[:, :, :], writes=[xT.reg(c, 4) for c in range(NCH)])
    kb.dma("sp", CIN.t[:], cT_d[:, :, :], writes=[CIN.reg()])
    kb.dma("sp", BADA.t[:], b_adaT_d[:, :, :], writes=[BADA.reg()])
    kb.dma("sp", LNP.t[:], lnT_d[:, :, :, :, :], writes=[LNP.reg()])
    kb.op("pool", lambda e: e.memset(ONES.t[:], 1.0 / D), writes=[ONES.reg()])
    kb.op("pool", lambda e: e.memset(ONES128.t[:], 1.0 / 128), writes=[ONES128.reg()])
    kb.op("pool", lambda e: e.memset(ONESB.t[:], 1.0), writes=[ONESB.reg()])
    if has_even:
        kb.dma("sp", IDF.t[:], idf_d[:, :], writes=[IDF.reg()])
        kb.dma("sp", MSK32.t[:], hgc_d[:, 0:256], writes=[MSK32.reg()])
        kb.dma("sp", HGSM.t[:], hgsm_d[:, :], writes=[HGSM.reg()])
        kb.dma("sp", HGLB.t[:], hglb_d[:, :, :, :], writes=[HGLB.reg()])
    if has_odd:
        kb.dma("sp", ROT.t[:], rotm_d[:, :], writes=[ROT.reg()])
        kb.dma("sp", DAL.t[:], dalam_d[:, :, :], writes=[DAL.reg()])
        kb.dma("sp", SUBWT.t[:], subw_d[:, :], writes=[SUBWT.reg()])
    kb.op("act", lambda e: e.activation(out=CA.t[:], in_=CIN.t[:], func=AF.Silu),
          reads=[CIN.reg()], writes=[CA.reg()])

    WREG = {}

    def issue_gather(L):
        r_b, r_g, bnc = gather_ev[L]
        kb.collective(lambda e: e.collective_compute(
            "AllGather", ALU.bypass, replica_groups=[list(range(N_CORES))],
            ins=[bnc.ap().opt()], outs=[WG[L].ap().opt()]), reads=[r_b], writes=[r_g])
        WREG[L] = r_g

    wbufs = [WA[0], WB[0], WA[1], WB[1]]
    pcnt = {"p": 0}

    def compute_mods(L):
        for pc in range(18):
            wb_ = wbufs[pcnt["p"] % 4]
            pcnt["p"] += 1
            kb.dma("pool", wb_.t[:, :, :], wv(L, "ada", 0, NCH, pc * 512, 512),
                   reads=[WREG[L]], writes=[wb_.reg()])
            pst = PS[pc % 2]
            for q in range(4):
                fcg = pc * 4 + q
                j, ch = fcg // 8, fcg % 8
                for kc in range(NCH):
                    kb.op("pe", lambda e: e.matmul(
                        pst.t[:, 2 * q:2 * q + 2], lhsT=wb_.t[:, kc, q * 128:(q + 1) * 128],
                        rhs=CA.t[:, kc, :], start=(kc == 0), stop=(kc == NCH - 1)),
                        reads=[wb_.reg(), CA.reg()], writes=[pst.reg()])
                kb.op("dve", lambda e: e.tensor_scalar(
                    out=MOD.t[:, L, ch, j, :], in0=pst.t[:, 2 * q:2 * q + 2],
                    scalar1=BADA.t[:, L, fcg:fcg + 1], scalar2=None, op0=ALU.add),
                    reads=[pst.reg(), BADA.reg()], writes=[MOD.reg(L)])

    def compute_der(L, part):
        rd = [MOD.reg(L), LNP.reg()] + ([MOD.reg(L + 1)] if part == "b" else [])
        wr = [DER.reg(L, part)]
        for s in range(2):
            def mod(LL, j):
                return MOD.t[:, LL, :, j, s]
            nxt = [(0, L, 3, 4), (1, L, 6, 7)] if part == "a" else [(2, L + 1, 0, 1)]
            for (k, LL, jsh, jsc) in nxt:
                A = DER.t[:, L, 2 * k, :, s]
                B = DER.t[:, L, 2 * k + 1, :, s]
                kb.op("dve", lambda e: e.tensor_scalar(
                    out=A, in0=mod(LL, jsc), scalar1=1.0, scalar2=None, op0=ALU.add), reads=rd, writes=wr)
                kb.op("dve", lambda e: e.tensor_tensor(
                    out=B, in0=A, in1=LNP.t[:, L, k, 1, :], op=ALU.mult), reads=rd, writes=wr)
                kb.op("dve", lambda e: e.tensor_tensor(
                    out=B, in0=B, in1=mod(LL, jsh), op=ALU.add), reads=rd, writes=wr)
                kb.op("dve", lambda e: e.tensor_tensor(
                    out=A, in0=A, in1=LNP.t[:, L, k, 0, :], op=ALU.mult), reads=rd, writes=wr)
            if part == "a":
                for k, (j, f) in enumerate([(2, 0.5 / ALPHA), (5, 1.0 / ALPHA), (8, 0.5 / ALPHA)]):
                    kb.op("dve", lambda e: e.tensor_scalar(
                        out=DER.t[:, L, 6 + k, :, s], in0=mod(L, j), scalar1=f, scalar2=None, op0=ALU.mult),
                        reads=rd, writes=wr)
                kb.op("dve", lambda e: e.tensor_scalar(
                    out=DER.t[:, L, 9, :, s], in0=mod(L, 1), scalar1=1.0, scalar2=None, op0=ALU.add),
                    reads=rd, writes=wr)

    L0 = layers[0]
    for L_ in layers:
        issue_gather(L_)
    allw = Reg()
    kb.op("pool", lambda e: e.memset(ONESB.t[:, :], 1.0), reads=[WREG[L_] for L_ in layers], writes=[ONESB.reg(), allw])
    for L_ in layers:
        WREG[L_] = allw
    compute_mods(L0)
    compute_der(L0, "a")

    def sidx(ti):
        return 1 if ti == 4 else 0

    for ti, (t0, n) in enumerate(TILES):
        s = sidx(ti)
        for c in range(NCH):
            kb.op("act", lambda e, c=c, t0=t0, n=n, s=s: e.activation(
                out=hT.t[:, c, t0:t0 + n], in_=xT.t[:, c, t0:t0 + n], func=AF.Identity,
                scale=DER.t[:, L0, 9, c, s:s + 1], bias=MOD.t[:, L0, c, 0, s:s + 1]),
                reads=[xT.reg(c, ti), DER.reg(L0, 'a'), MOD.reg(L0)], writes=[hT.reg(c, ti)])

    GROUPS = [(0, 4), (4, 8), (8, 12), (12, 16), (16, 20), (20, 22)]
    cnt = {"g": 0, "gt": 0, "sil": 0, "tmp": 0}

    def ffn(L, which, tiles):
        gsk = 6 if which == 0 else 8
        for (j0, j1) in GROUPS:
            ng = j1 - j0
            gi = cnt["g"]
            cnt["g"] += 1
            wa, wb_, wo = WA[gi % 2], WB[gi % 2], WO[gi % 2]
            fin, fout = f"fin{which}", f"fout{which}"
            kb.dma("pool", wa.t[:, :, 0:ng * 128], wv(L, fin, 0, NCH, j0 * 128, ng * 128),
                   reads=[WREG[L]], writes=[wa.reg()])
            kb.dma("pool", wb_.t[:, :, 0:ng * 128], wv(L, fin, 0, NCH, FH + j0 * 128, ng * 128),
                   reads=[WREG[L]], writes=[wb_.reg()])
            kb.dma("pool", wo.t[:, 0:ng, :], wv(L, fout, j0, ng, 0, D),
                   reads=[WREG[L]], writes=[wo.reg()])
            for ti in tiles:
                t0, n = TILES[ti]
                s = sidx(ti)
                gt = GT[cnt["gt"] % 2]
                cnt["gt"] += 1
                for jj in range(ng):
                    pa, pb = PS[(2 * jj) % 4], PS[(2 * jj + 1) % 4]
                    for (pp, wsrc) in ((pa, wa), (pb, wb_)):
                        for kc in range(NCH):
                            kb.op("pe", lambda e, pp=pp, wsrc=wsrc, kc=kc, jj=jj, t0=t0, n=n: e.matmul(
                                pp.t[:, 0:n], lhsT=wsrc.t[:, kc, jj * 128:(jj + 1) * 128],
                                rhs=hT.t[:, kc, t0:t0 + n], start=(kc == 0), stop=(kc == NCH - 1)),
                                reads=[wsrc.reg(), hT.reg(kc, ti)], writes=[pp.reg()])
                    sl = SIL[cnt["sil"] % 2]
                    cnt["sil"] += 1
                    kb.op("act", lambda e, sl=sl, pa=pa, n=n: e.activation(
                        out=sl.t[:, 0:n], in_=pa.t[:, 0:n], func=AF.Silu),
                        reads=[pa.reg()], writes=[sl.reg()])
                    kb.op("dve", lambda e, sl=sl, pb=pb, gt=gt, jj=jj, n=n: e.tensor_tensor(
                        out=gt.t[:, jj, 0:n], in0=sl.t[:, 0:n], in1=pb.t[:, 0:n], op=ALU.mult),
                        reads=[sl.reg(), pb.reg()], writes=[gt.reg(jj)])
                for dc in range(NCH):
                    py = PS[4 + dc % 4]
                    for jj in range(ng):
                        kb.op("pe", lambda e, py=py, wo=wo, jj=jj, dc=dc, gt=gt, n=n: e.matmul(
                            py.t[:, 0:n], lhsT=wo.t[:, jj, dc * 128:(dc + 1) * 128],
                            rhs=gt.t[:, jj, 0:n], start=(jj == 0), stop=(jj == ng - 1)),
                            reads=[wo.reg(), gt.reg(jj)], writes=[py.reg()])
                    kb.op("dve", lambda e, py=py, dc=dc, t0=t0, n=n, s=s: e.scalar_tensor_tensor(
                        out=xT.t[:, dc, t0:t0 + n], in0=py.t[:, 0:n],
                        scalar=DER.t[:, L, gsk, dc, s:s + 1], in1=xT.t[:, dc, t0:t0 + n],
                        op0=ALU.mult, op1=ALU.add),
                        reads=[py.reg(), DER.reg(L, 'a'), xT.reg(dc, ti)], writes=[xT.reg(dc, ti)])

    def ln_pass(L, k, tiles, want_h=True):
        eps = LN_EPS / (ALPHA * ALPHA)
        for ti in tiles:
            t0, n = TILES[ti]
            s = sidx(ti)
            xr = [xT.reg(c, ti) for c in range(NCH)]
            pm, pe2 = PS[0], PS[1]
            for c in range(NCH):
                kb.op("pe", lambda e, c=c, t0=t0, n=n: e.matmul(
                    pm.t[:, 0:n], lhsT=ONES.t[:, :], rhs=xT.t[:, c, t0:t0 + n],
                    start=(c == 0), stop=(c == NCH - 1)),
                    reads=[ONES.reg(), xT.reg(c, ti)], writes=[pm.reg()])
            for c in range(NCH):
                kb.op("act", lambda e, c=c, t0=t0, n=n: e.activation(
                    out=SQ.t[:, c % 3, 0:n], in_=xT.t[:, c, t0:t0 + n], func=AF.Square),
                    reads=[xT.reg(c, ti)], writes=[SQ.reg(c % 3)])
                kb.op("pe", lambda e, c=c, n=n: e.matmul(
                    pe2.t[:, 0:n], lhsT=ONES.t[:, :], rhs=SQ.t[:, c % 3, 0:n],
                    start=(c == 0), stop=(c == NCH - 1)),
                    reads=[ONES.reg(), SQ.reg(c % 3)], writes=[pe2.reg()])
            msq, var, rstd = STAT
            kb.op("act", lambda e, n=n: e.activation(out=msq.t[:, 0:n], in_=pm.t[:, 0:n], func=AF.Square),
                  reads=[pm.reg()], writes=[msq.reg()])
            kb.op("dve", lambda e, n=n: e.tensor_tensor(
                out=var.t[:, 0:n], in0=pe2.t[:, 0:n], in1=msq.t[:, 0:n], op=ALU.subtract),
                reads=[pe2.reg(), msq.reg()], writes=[var.reg()])
            kb.op("dve", lambda e, n=n: e.tensor_scalar(
                out=var.t[:, 0:n], in0=var.t[:, 0:n], scalar1=eps, scalar2=None, op0=ALU.add),
                reads=[var.reg()], writes=[var.reg()])
            kb.op("act", lambda e, n=n: e.activation(out=var.t[:, 0:n], in_=var.t[:, 0:n], func=AF.Sqrt),
                  reads=[var.reg()], writes=[var.reg()])
            kb.op("dve", lambda e, n=n: e.reciprocal(out=rstd.t[:, 0:n], in_=var.t[:, 0:n]),
                  reads=[var.reg()], writes=[rstd.reg()])
            for c in range(NCH):
                tm = TMP[cnt["tmp"] % 2]
                cnt["tmp"] += 1
                kb.op("dve", lambda e, tm=tm, c=c, t0=t0, n=n: e.tensor_tensor(
                    out=tm.t[:, 0:n], in0=xT.t[:, c, t0:t0 + n], in1=pm.t[:, 0:n], op=ALU.subtract),
                    reads=[xT.reg(c, ti), pm.reg()], writes=[tm.reg()])
                kb.op("dve", lambda e, tm=tm, n=n: e.tensor_tensor(
                    out=tm.t[:, 0:n], in0=tm.t[:, 0:n], in1=rstd.t[:, 0:n], op=ALU.mult),
                    reads=[tm.reg(), rstd.reg()], writes=[tm.reg()])
                kb.op("act", lambda e, tm=tm, c=c, t0=t0, n=n: e.activation(
                    out=xT.t[:, c, t0:t0 + n], in_=tm.t[:, 0:n], func=AF.Identity,
                    scale=LNP.t[:, L, k, 0, c:c + 1], bias=LNP.t[:, L, k, 1, c:c + 1]),
                    reads=[tm.reg(), LNP.reg()], writes=[xT.reg(c, ti)])
                if want_h:
                    kb.op("act", lambda e, tm=tm, c=c, t0=t0, n=n, s=s: e.activation(
                        out=hT.t[:, c, t0:t0 + n], in_=tm.t[:, 0:n], func=AF.Identity,
                        scale=DER.t[:, L, 2 * k, c, s:s + 1], bias=DER.t[:, L, 2 * k + 1, c, s:s + 1]),
                        reads=[tm.reg(), DER.reg(L, 'a' if k < 2 else 'b')], writes=[hT.reg(c, ti)])


    AX = mybir.AxisListType.X

    def mm(out, lhsT, rhs, start, stop, reads, writes):
        kb.op("pe", lambda e: e.matmul(out, lhsT=lhsT, rhs=rhs, start=start, stop=stop), reads=reads, writes=writes)

    def attn_mixer(L, tiles):
        o = L // 2
        last = L == DEPTH - 1
        lam_init = 0.8 - 0.6 * math.exp(-0.3 * L)
        for i in range(2):
            kb.op("dve", lambda e: e.tensor_tensor(out=LTMP.t[:, :], in0=DAL.t[:, o, 128 * i:128 * i + 64],
                                                   in1=DAL.t[:, o, 128 * i + 64:128 * i + 128], op=ALU.mult),
                  reads=[DAL.reg()], writes=[LTMP.reg()])
            kb.op("dve", lambda e: e.reduce_sum(out=LSC.t[:, i:i + 1], in_=LTMP.t[:, :], axis=AX),
                  reads=[LTMP.reg()], writes=[LSC.reg()])
        kb.op("act", lambda e: e.activation(out=LSC.t[:, 2:4], in_=LSC.t[:, 0:2], func=AF.Exp),
              reads=[LSC.reg()], writes=[LSC.reg()])
        kb.op("dve", lambda e: e.tensor_tensor(out=LSC.t[:, 4:5], in0=LSC.t[:, 3:4], in1=LSC.t[:, 2:3],
                                               op=ALU.subtract), reads=[LSC.reg()], writes=[LSC.reg()])
        kb.op("dve", lambda e: e.tensor_scalar(out=LSC.t[:, 4:5], in0=LSC.t[:, 4:5], scalar1=-lam_init,
                                               scalar2=None, op0=ALU.add), reads=[LSC.reg()], writes=[LSC.reg()])
        kb.op("dve", lambda e: e.tensor_scalar(out=LSC.t[:, 5:6], in0=SUBWT.t[:, o:o + 1], scalar1=1.0 - lam_init,
                                               scalar2=None, op0=ALU.mult),
              reads=[SUBWT.reg(), LSC.reg()], writes=[LSC.reg()])
        NL = LSC.t[:, 4:5]
        SUBW = LSC.t[:, 5:6]

        kb.alias(ATT_SCR, FFN_SCR)
        QDv = QD.ap().rearrange("p (h t) -> p h t", h=8)
        KSv = KS.ap().rearrange("(h p) t -> p h t", p=128)
        KCv = KC.ap().rearrange("p (h t) -> p h t", h=8)
        VSv = VS.ap().rearrange("(h p) (k d) -> p h k d", p=128, k=16)
        VCv = VC.ap().rearrange("p (h k d) -> p h k d", h=8, k=2)
        KGv = KG.ap().rearrange("(h r p) t -> h r p t", r=2, p=128)
        VGv = VG.ap().rearrange("(h r p) (k d) -> h r p k d", r=2, p=128, k=16)

        wq = [WA[0], WB[0], WA[1], WB[1]]
        for i, wbuf in enumerate(wq):
            kb.dma("pool", wbuf.t[:, :, :], wv(L, "min", 0, NCH, i * 512, 512), reads=[WREG[L]], writes=[wbuf.reg()])
        COS, SIN = SIL
        for ti, (t0, n) in enumerate(TILES):
            if ti < 4:
                kb.dma("sp", COS.t[:, 0:n], rope_d[:, 0, t0:t0 + n], writes=[COS.reg()])
                kb.dma("sp", SIN.t[:, 0:n], rope_d[:, 1, t0:t0 + n], writes=[SIN.reg()])
            for ch in range(16):
                wbuf, cc = wq[ch // 4], ch % 4
                h = ch % 8
                pp = PS[ch % 2]
                for kc in range(NCH):
                    mm(pp.t[:, 0:n], wbuf.t[:, kc, cc * 128:(cc + 1) * 128], hT.t[:, kc, t0:t0 + n],
                       kc == 0, kc == NCH - 1, [wbuf.reg(), hT.reg(kc, ti)], [pp.reg()])
                stg = EB[ch % 4]
                scale = 0.125 if ch < 8 else 1.0
                if ti < 4:
                    a32, pr = TMP[ch % 2], PS[2 + ch % 2]
                    t1, t2 = STAT[ch % 2], SQ
                    kb.op("act", lambda e: e.activation(out=a32.t[:, 0:n], in_=pp.t[:, 0:n], func=AF.Copy,
                                                        scale=scale), reads=[pp.reg()], writes=[a32.reg()])
                    mm(pr.t[:, 0:n], ROT.t[:, :], a32.t[:, 0:n], True, True, [ROT.reg(), a32.reg()], [pr.reg()])
                    kb.op("dve", lambda e: e.tensor_tensor(out=t1.t[:, 0:n], in0=a32.t[:, 0:n], in1=COS.t[:, 0:n],
                                                           op=ALU.mult),
                          reads=[a32.reg(), COS.reg()], writes=[t1.reg()])
                    kb.op("dve", lambda e: e.tensor_tensor(out=t2.t[:, ch % 2, 0:n], in0=pr.t[:, 0:n],
                                                           in1=SIN.t[:, 0:n], op=ALU.mult),
                          reads=[pr.reg(), SIN.reg()], writes=[t2.reg(ch % 2)])
                    kb.op("pool", lambda e: e.tensor_tensor(out=stg.t[:, 0:n], in0=t1.t[:, 0:n],
                                                            in1=t2.t[:, ch % 2, 0:n], op=ALU.add),
                          reads=[t1.reg(), t2.reg(ch % 2)], writes=[stg.reg()])
                else:
                    kb.op("act", lambda e: e.activation(out=stg.t[:, 0:n], in_=pp.t[:, 0:n], func=AF.Copy,
                                                        scale=scale), reads=[pp.reg()], writes=[stg.reg()])
                if ch < 8:
                    dst, dreg = QDv[:, h, t0:t0 + n], DREG.reg("q", h, ti)
                elif ti < 4:
                    dst, dreg = KSv[:, h, t0:t0 + n], DREG.reg("ks", h, ti)
                else:
                    dst, dreg = KCv[:, h, :], DREG.reg("kc", h)
                kb.dma("sp", dst, stg.t[:, 0:n], reads=[stg.reg()], writes=[dreg])

        for i in range(2):
            kb.dma("pool", wq[i].t[:, :, :], wv(L, "min", 0, NCH, 2048 + i * 512, 512),
                   reads=[WREG[L]], writes=[wq[i].reg()])
        for tb in range(18):
            tok0 = tb * 128
            ti = min(tok0 // 512, 4)
            for i in range(2):
                pp = PS[(2 * tb + i) % 4]
                for kc in range(NCH):
                    mm(pp.t[:, 0:512], hT.t[:, kc, tok0:tok0 + 128], wq[i].t[:, kc, :], kc == 0, kc == NCH - 1,
                       [wq[i].reg(), hT.reg(kc, ti)], [pp.reg()])
                stg = EB[(2 * tb + i) % 4]
                eng = "act" if i == 0 else "dve"
                if eng == "act":
                    kb.op("act", lambda e: e.activation(out=stg.t[:, :], in_=pp.t[:, :], func=AF.Copy),
                          reads=[pp.reg()], writes=[stg.reg()])
                else:
                    kb.op("dve", lambda e: e.tensor_copy(out=stg.t[:, :], in_=pp.t[:, :]),
                          reads=[pp.reg()], writes=[stg.reg()])
                if tb < 16:
                    dst, dreg = VSv[:, 4 * i:4 * i + 4, tb, :], DREG.reg("vs", tb, i)
                else:
                    dst, dreg = VCv[:, 4 * i:4 * i + 4, tb - 16, :], DREG.reg("vc", tb - 16, i)
                kb.dma("sp", dst, stg.t[:, :].rearrange("p (h d) -> p h d", h=4), reads=[stg.reg()], writes=[dreg])

        if stop == "1p":
            raise StopBuild
        pairs = [[2 * i, 2 * i + 1] for i in range(N_CORES // 2)]
        for h in range(8):
            kb.collective(lambda e: e.collective_compute(
                "AllGather", ALU.bypass, replica_groups=pairs,
                ins=[KS.ap()[h * 128:(h + 1) * 128, :].opt()], outs=[KG.ap()[h * 256:(h + 1) * 256, :].opt()]),
                reads=[DREG.reg("ks", h, ti) for ti in range(4)], writes=[DREG.reg("kg", h)])
            kb.collective(lambda e: e.collective_compute(
                "AllGather", ALU.bypass, replica_groups=pairs,
                ins=[VS.ap()[h * 128:(h + 1) * 128, :].opt()], outs=[VG.ap()[h * 256:(h + 1) * 256, :].opt()]),
                reads=[DREG.reg("vs", tb, h // 4) for tb in range(16)], writes=[DREG.reg("vg", h)])
        if stop == "1g":
            raise StopBuild
        qtiles = [(ti, TILES[ti][0], TILES[ti][1], list(range(34))) for ti in range(4)]
        if not last:
            qtiles.append((4, TL, TC, [0, 1]))
        for sub in range(2):
            lo, hi = 64 * (1 - sub), 64 * (1 - sub) + 64
            kb.op("pool", lambda e: e.memset(QH[sub].t[lo:hi, :], 0.0), writes=[QH[sub].reg("z")])
        for h in range(8):
            kb.dma("sp", KH.t[:, 0:TC], KCv[:, h, :], reads=[DREG.reg("kc", h)], writes=[KH.reg(0)])
            kb.dma("sp", VH.t[:, 0:2, :], VCv[:, h, :, :],
                   reads=[DREG.reg("vc", k, i) for k in range(2) for i in range(2)], writes=[VH.reg(0)])
            for r in range(2):
                kb.dma("sp", KH.t[:, TC + r * TL:TC + (r + 1) * TL], KGv[h, r, :, :],
                       reads=[DREG.reg("kg", h)], writes=[KH.reg(1 + r)])
                kb.dma("sp", VH.t[:, 2 + 16 * r:18 + 16 * r, :], VGv[h, r, :, :, :],
                       reads=[DREG.reg("vg", h)], writes=[VH.reg(1 + r)])
            for sub in range(2):
                lo, hi = 64 * sub, 64 * sub + 64
                kb.dma("sp", QH[sub].t[lo:hi, :], QDv[lo:hi, h, :], reads=[DREG.reg("q", h, ti) for ti in range(5)],
                       writes=[QH[sub].reg("d")])

            def part(kt):
                return 0 if kt < 2 else (1 if kt < 18 else 2)

            if stop == "1l":
                raise StopBuild

            for (ti, q0, n, kts) in qtiles:
                nk = len(kts)

                def smm(idx):
                    kt = kts[idx]
                    for sub in range(2):
                        ps = PS[2 * sub + idx % 2]
                        mm(ps.t[:, 0:n], KH.t[:, kt * 128:(kt + 1) * 128], QH[sub].t[:, q0:q0 + n], True, True,
                           [KH.reg(part(kt)), QH[sub].reg("d"), QH[sub].reg("z")], [ps.reg()])

                smm(0)
                for idx, kt in enumerate(kts):
                    if idx + 1 < nk:
                        smm(idx + 1)
                    for sub in range(2):
                        ps = PS[2 * sub + idx % 2]
                        eb = EB[(2 * idx + sub) % 4]
                        kb.op("act", lambda e: e.activation(out=eb.t[:, 0:n], in_=ps.t[:, 0:n], func=AF.Exp),
                              reads=[ps.reg()], writes=[eb.reg()])
                        mm(PS[4 + sub].t[:, 0:n], VH.t[:, kt, :], eb.t[:, 0:n], idx == 0, idx == nk - 1,
                           [VH.reg(part(kt)), eb.reg()], [PS[4 + sub].reg()])
                        mm(PS[6 + sub].t[:, 0:n], ONESB.t[:, :], eb.t[:, 0:n], idx == 0, idx == nk - 1,
                           [ONESB.reg(), eb.reg()], [PS[6 + sub].reg()])
                if stop == "1m":
                    raise StopBuild
                R0, R1, OO = STAT
                T0, T1 = TMP
                for sub, (R, T) in enumerate(((R0, T0), (R1, T1))):
                    kb.op("dve", lambda e: e.reciprocal(out=R.t[:, 0:n], in_=PS[6 + sub].t[:, 0:n]),
                          reads=[PS[6 + sub].reg()], writes=[R.reg()])
                    kb.op("dve", lambda e: e.tensor_tensor(out=T.t[:, 0:n], in0=PS[4 + sub].t[:, 0:n],
                                                           in1=R.t[:, 0:n], op=ALU.mult),
                          reads=[PS[4 + sub].reg(), R.reg()], writes=[T.reg()])
                kb.op("dve", lambda e: e.scalar_tensor_tensor(out=OO.t[:, 0:n], in0=T1.t[:, 0:n], scalar=NL,
                                                              in1=T0.t[:, 0:n], op0=ALU.mult, op1=ALU.add),
                      reads=[T0.reg(), T1.reg(), LSC.reg()], writes=[OO.reg()])
                kb.op("act", lambda e: e.activation(out=SQ.t[:, 2, 0:n], in_=OO.t[:, 0:n], func=AF.Square),
                      reads=[OO.reg()], writes=[SQ.reg(2)])
                pss = PS[0]
                mm(pss.t[:, 0:n], ONES128.t[:, :], SQ.t[:, 2, 0:n], True, True, [ONES128.reg(), SQ.reg(2)], [pss.reg()])
                kb.op("dve", lambda e: e.tensor_scalar(out=R0.t[:, 0:n], in0=pss.t[:, 0:n], scalar1=RMS_EPS,
                                                       scalar2=None, op0=ALU.add),
                      reads=[pss.reg()], writes=[R0.reg()])
                kb.op("act", lambda e: e.activation(out=R0.t[:, 0:n], in_=R0.t[:, 0:n], func=AF.Sqrt),
                      reads=[R0.reg()], writes=[R0.reg()])
                kb.op("dve", lambda e: e.reciprocal(out=R1.t[:, 0:n], in_=R0.t[:, 0:n]),
                      reads=[R0.reg()], writes=[R1.reg()])
                kb.op("dve", lambda e: e.tensor_tensor(out=T0.t[:, 0:n], in0=OO.t[:, 0:n], in1=R1.t[:, 0:n],
                                                       op=ALU.mult), reads=[OO.reg(), R1.reg()], writes=[T0.reg()])
                kb.op("act", lambda e: e.activation(out=hT.t[:, h, q0:q0 + n], in_=T0.t[:, 0:n], func=AF.Identity,
                                                    scale=SUBW), reads=[T0.reg(), LSC.reg()], writes=[hT.reg(h, ti)])

            if stop == "1h":
                raise StopBuild
        kb.alias(FFN_SCR, ATT_SCR)
        out_proj(L, tiles)

    def out_proj(L, tiles):
        for i in range(2):
            kb.dma("pool", WA[i].t[:, :, :], wv(L, "mout", 0, NCH, i * 512, 512), reads=[WREG[L]],
                   writes=[WA[i].reg()])
        for ti in tiles:
            t0, n = TILES[ti]
            s = sidx(ti)
            for dc in range(NCH):
                py = PS[4 + dc % 4]
                wbuf, cc = WA[dc // 4], dc % 4
                for kc in range(NCH):
                    mm(py.t[:, 0:n], wbuf.t[:, kc, cc * 128:(cc + 1) * 128], hT.t[:, kc, t0:t0 + n],
                       kc == 0, kc == NCH - 1, [wbuf.reg(), hT.reg(kc, ti)], [py.reg()])
                kb.op("dve", lambda e: e.scalar_tensor_tensor(
                    out=xT.t[:, dc, t0:t0 + n], in0=py.t[:, 0:n], scalar=DER.t[:, L, 7, dc, s:s + 1],
                    in1=xT.t[:, dc, t0:t0 + n], op0=ALU.mult, op1=ALU.add),
                    reads=[py.reg(), DER.reg(L, 'a'), xT.reg(dc, ti)], writes=[xT.reg(dc, ti)])

    def hgrn_mixer(L, tiles):
        e = L // 2
        kb.alias(HG_SCR, FFN_SCR)
        HDK = 128 ** -0.5
        VDv = VD.ap().rearrange("p (h k v) -> p h k v", h=4, k=18)
        YPDv = YPD.ap().rearrange("p (g t) -> p g t", g=4)
        OADv = OAD.ap().rearrange("p (h t) -> p h t", h=4)
        RECDv = RECD.ap().rearrange("p (h t) -> p h t", h=4)
        MASK = [MSK32.t[:, 0:128], MSK32.t[:, 128:256]]
        RSTv = HGCB.t[:, 256:768]
        kb.dma("pool", HGCB.t[:, :], hgc_d[:, :], writes=[HGCB.reg()])
        kb.dma("pool", CM2.t[:, :, :], cm2_d.rearrange("p (c t) -> p c t", c=4), writes=[CM2.reg()])
        kb.op("act", lambda en: en.activation(out=IDB.t[:, :], in_=IDF.t[:, :], func=AF.Copy), reads=[IDF.reg()], writes=[IDB.reg()])
        kb.dma("pool", BANDB.t[:, :, :, :], band_d.rearrange("p (k g t) -> p k g t", k=6, g=4), writes=[BANDB.reg()])
        kb.dma("pool", PWB.t[:, :, :], wv(L, "poolw", 0, 4, 0, 128), reads=[WREG[L]], writes=[PWB.reg()])
        kb.op("pool", lambda en: en.memset(UHB.t[:, :], 0.0), writes=[UHB.reg()])
        if e == 0:
            for k_ in range(2):
                kb.op("pool", lambda en: en.memset(LBB.t[:, 2 * k_, :], 0.0), writes=[LBB.reg()])
                kb.op("pool", lambda en: en.memset(LBB.t[:, 2 * k_ + 1, :], 1.0), writes=[LBB.reg()])
        else:
            T8 = STT_T[0].t[:, 0:8]
            kb.op("dve", lambda en: en.tensor_tensor(out=T8.rearrange("p (a b) -> p a b", a=2), in0=HGLB.t[:, 1, :, :],
                                                     in1=HGLB.t[:, 0, :, :], op=ALU.subtract),
                  reads=[HGLB.reg()], writes=[STT_T[0].reg()])
            kb.op("act", lambda en: en.activation(out=T8, in_=T8, func=AF.Sigmoid),
                  reads=[STT_T[0].reg()], writes=[STT_T[0].reg()])
            for k_ in range(2):
                kb.op("dve", lambda en: en.tensor_scalar(out=LBB.t[:, 2 * k_, :], in0=T8[:, 0:4],
                                                         scalar1=HGSM.t[:, 2 * k_:2 * k_ + 1], scalar2=None, op0=ALU.mult),
                      reads=[STT_T[0].reg(), HGSM.reg()], writes=[LBB.reg()])
                kb.op("dve", lambda en: en.scalar_tensor_tensor(out=LBB.t[:, 2 * k_, :], in0=T8[:, 4:8],
                                                                scalar=HGSM.t[:, 2 * k_ + 1:2 * k_ + 2],
                                                                in1=LBB.t[:, 2 * k_, :], op0=ALU.mult, op1=ALU.add),
                      reads=[STT_T[0].reg(), HGSM.reg(), LBB.reg()], writes=[LBB.reg()])
                kb.op("dve", lambda en: en.tensor_scalar(out=LBB.t[:, 2 * k_ + 1, :], in0=LBB.t[:, 2 * k_, :],
                                                         scalar1=-1.0, scalar2=1.0, op0=ALU.mult, op1=ALU.add),
                      reads=[LBB.reg()], writes=[LBB.reg()])

        kb.dma("pool", WA[0].t[:, :, :], wv(L, "min", 0, NCH, 512, 512), reads=[WREG[L]], writes=[WA[0].reg()])
        kb.dma("pool", WB[0].t[:, :, :], wv(L, "min", 0, NCH, 2560, 512), reads=[WREG[L]], writes=[WB[0].reg()])
        for tb in range(18):
            tok0 = tb * 128
            ti = min(tok0 // 512, 4)
            pp = PS[tb % 2]
            for kc in range(NCH):
                mm(pp.t[:, 0:512], hT.t[:, kc, tok0:tok0 + 128], WA[0].t[:, kc, :], kc == 0, kc == NCH - 1,
                   [WA[0].reg(), hT.reg(kc, ti)], [pp.reg()])
            stg = STG[tb % 2]
            kb.op("act", lambda en: en.activation(out=stg.t[:, :], in_=pp.t[:, :], func=AF.Copy),
                  reads=[pp.reg()], writes=[stg.reg()])
            kb.dma("sp", VDv[:, :, tb, :], stg.t[:, :].rearrange("p (h d) -> p h d", h=4),
                   reads=[stg.reg()], writes=[EREG.reg("vd", tb)])

        if stop == f"{L}e":
            raise StopBuild
        def u_block(tb):
            tok0 = tb * 128
            ti = min(tok0 // 512, 4)
            pp = PS[2 + tb % 2]
            for kc in range(NCH):
                mm(pp.t[:, 0:512], hT.t[:, kc, tok0:tok0 + 128], WB[0].t[:, kc, :], kc == 0, kc == NCH - 1,
                   [WB[0].reg(), hT.reg(kc, ti)], [pp.reg()])
            kb.op("dve", lambda en: en.tensor_copy(out=UT.t[:, tb % 3, :], in_=pp.t[:, :]),
                  reads=[pp.reg()], writes=[UT.reg(tb % 3)])

        def pool_block(tb, srcs):
            tok0 = tb * 128
            pq = PS[4 + tb % 2]
            for g in range(4):
                for i, (reg_, apf, kind) in enumerate(srcs):
                    mm(pq.t[:, g * 128:(g + 1) * 128], apf(g), BANDB.t[:, kind, g, :], i == 0, i == len(srcs) - 1,
                       [reg_, BANDB.reg()], [pq.reg()])
            kb.op("dve", lambda en: en.tensor_copy(out=PL.t[:, :, :], in_=pq.t[:, :].rearrange("p (g t) -> p g t", g=4)),
                  reads=[pq.reg()], writes=[PL.reg()])
            py = PS[6 + tb % 2]
            for g in range(4):
                mm(py.t[:, g * 128:(g + 1) * 128], PWB.t[:, g, :], PL.t[:, g, :], True, True,
                   [PWB.reg(), PL.reg()], [py.reg()])
            stg = STG[tb % 2]
            for g in range(4):
                kb.op("act", lambda en: en.activation(out=stg.t[:, g * 128:(g + 1) * 128], in_=py.t[:, g * 128:(g + 1) * 128],
                                                      func=AF.Identity, scale=HGSM.t[:, 8 + 4 * e + g:9 + 4 * e + g]),
                      reads=[py.reg(), HGSM.reg()], writes=[stg.reg()])
            kb.dma("sp", YPDv[:, :, tok0:tok0 + 128], stg.t[:, :].rearrange("p (g t) -> p g t", g=4),
                   reads=[stg.reg()], writes=[EREG.reg("ypd", tb)])

        def usrc(tb):
            return (UT.reg(tb % 3), lambda g: UT.t[:, tb % 3, g * 128:(g + 1) * 128])

        u_block(16)
        u_block(17)
        pool_block(16, [usrc(16) + (3,), usrc(17) + (2,)])
        pool_block(17, [usrc(16) + (0,), usrc(17) + (4,)])
        u_block(0)
        for tb in range(15):
            u_block(tb + 1)
            srcs = ([usrc(tb - 1) + (0,)] if tb > 0 else []) + [usrc(tb) + (3 if tb == 0 else 1,), usrc(tb + 1) + (2,)]
            pool_block(tb, srcs)
        kb.dma("sp", HS.ap()[0:8, :], UT.t[120:128, 15 % 3, :], reads=[UT.reg(15 % 3)], writes=[EREG.reg("hs")])

        if stop == f"{L}u":
            raise StopBuild
        kb.dma("pool", WA[0].t[:, :, :], wv(L, "min", 0, NCH, 0, 512), reads=[WREG[L]], writes=[WA[0].reg()])
        kb.dma("pool", WA[1].t[:, :, :], wv(L, "min", 0, NCH, 1536, 512), reads=[WREG[L]], writes=[WA[1].reg()])
        kb.dma("pool", WB[0].t[:, :, :], wv(L, "min", 0, NCH, 2048, 512), reads=[WREG[L]], writes=[WB[0].reg()])
        kb.dma("pool", WB[1].t[:, :, :], wv(L, "min", 0, NCH, 1024, 512), reads=[WREG[L]], writes=[WB[1].reg()])
        cstate = {"n": 0}

        def scan_phase(dirn):
            bwd = dirn == 1
            sfw, sbw = HGSM.t[:, 2 * dirn:2 * dirn + 1], HGSM.t[:, 2 * dirn + 1:2 * dirn + 2]
            if not bwd:
                order = [(4, True), (0, False), (1, False), (2, False), (3, False)]
            else:
                order = [(3, "rx"), (2, False), (1, False), (0, False), (4, True)]
            for h in range(4):
                kb.dma("sp", VHD.t[:, :, :], VDv[:, h, :, :], reads=[EREG.reg("vd", tb) for tb in range(18)],
                       writes=[VHD.reg()])
                S = SST.t[:, h, :]
                for (ti, init) in order:
                    t0, n = TILES[ti]
                    nch = n // 32
                    if init is True:
                        kb.op("pool", lambda en: en.memset(S, 0.0), writes=[SST.reg(h)])
                    elif init == "rx":
                        kb.op("dve", lambda en: en.tensor_copy(out=S, in_=SRX.t[:, h * 128:(h + 1) * 128]),
                              reads=[SRX.reg()], writes=[SST.reg(h)])
                    pq, pf, pb = PS[0], PS[1], PS[2]
                    for (pp, wbuf) in ((pq, WA[0]), (pf, WA[1]), (pb, WB[0])):
                        for kc in range(NCH):
                            mm(pp.t[:, 0:n], wbuf.t[:, kc, h * 128:(h + 1) * 128], hT.t[:, kc, t0:t0 + n],
                               kc == 0, kc == NCH - 1, [wbuf.reg(), hT.reg(kc, ti)], [pp.reg()])
                    Z, KK, LF, B_, E_, EI = TMP[0], TMP[1], SIL[0], SIL[1], STAT[0], STAT[1]
                    KF = STAT[2]
                    kb.op("dve", lambda en: en.tensor_scalar(out=Z.t[:, 0:n], in0=pf.t[:, 0:n], scalar1=sfw, scalar2=None,
                                                             op0=ALU.mult), reads=[pf.reg(), HGSM.reg()], writes=[Z.reg()])
                    kb.op("dve", lambda en: en.scalar_tensor_tensor(out=Z.t[:, 0:n], in0=pb.t[:, 0:n], scalar=sbw,
                                                                    in1=Z.t[:, 0:n], op0=ALU.mult, op1=ALU.add),
                          reads=[pb.reg(), HGSM.reg(), Z.reg()], writes=[Z.reg()])
                    kb.op("act", lambda en: en.activation(out=Z.t[:, 0:n], in_=Z.t[:, 0:n], func=AF.Sigmoid),
                          reads=[Z.reg()], writes=[Z.reg()])
                    kb.op("dve", lambda en: en.tensor_scalar(out=Z.t[:, 0:n], in0=Z.t[:, 0:n],
                                                             scalar1=LBB.t[:, 2 * dirn + 1, h:h + 1],
                                                             scalar2=LBB.t[:, 2 * dirn, h:h + 1], op0=ALU.mult, op1=ALU.add),
                          reads=[Z.reg(), LBB.reg()], writes=[Z.reg()])
                    kb.op("dve", lambda en: en.tensor_scalar(out=KK.t[:, 0:n], in0=Z.t[:, 0:n], scalar1=-1.0, scalar2=1.0,
                                                             op0=ALU.mult, op1=ALU.add), reads=[Z.reg()], writes=[KK.reg()])
                    kb.op("dve", lambda en: en.tensor_scalar(out=Z.t[:, 0:n], in0=Z.t[:, 0:n], scalar1=F_TINY, scalar2=None,
                                                             op0=ALU.max), reads=[Z.reg(), KK.reg()], writes=[Z.reg()])
                    kb.op("act", lambda en: en.activation(out=LF.t[:, 0:n], in_=Z.t[:, 0:n], func=AF.Ln),
                          reads=[Z.reg()], writes=[LF.reg()])
                    cur = LF
                    for si, k_ in enumerate((1, 2, 4, 8, 16)):
                        dst = B_ if si % 2 == 0 else E_
                        c3 = cur.t[:, 0:n].rearrange("p (c j) -> p c j", j=32)
                        d3 = dst.t[:, 0:n].rearrange("p (c j) -> p c j", j=32)
                        if not bwd:
                            kb.op("act", lambda en: en.activation(out=d3[:, :, 0:k_], in_=c3[:, :, 0:k_], func=AF.Copy),
                                  reads=[cur.reg()], writes=[dst.reg()])
                            kb.op("dve", lambda en: en.tensor_tensor(out=d3[:, :, k_:32], in0=c3[:, :, k_:32], in1=c3[:, :, 0:32 - k_],
                                                                     op=ALU.add), reads=[cur.reg()], writes=[dst.reg()])
                        else:
                            kb.op("act", lambda en: en.activation(out=d3[:, :, 32 - k_:32], in_=c3[:, :, 32 - k_:32], func=AF.Copy),
                                  reads=[cur.reg()], writes=[dst.reg()])
                            kb.op("dve", lambda en: en.tensor_tensor(out=d3[:, :, 0:32 - k_], in0=c3[:, :, 0:32 - k_], in1=c3[:, :, k_:32],
                                                                     op=ALU.add), reads=[cur.reg()], writes=[dst.reg()])
                        cur = dst
                    if stop == f"{L}q":
                        raise StopBuild
                    kb.op("act", lambda en: en.activation(out=E_.t[:, 0:n], in_=B_.t[:, 0:n], func=AF.Exp),
                          reads=[B_.reg()], writes=[E_.reg()])
                    kb.op("act", lambda en: en.activation(out=EI.t[:, 0:n], in_=B_.t[:, 0:n], func=AF.Exp, scale=-1.0),
                          reads=[B_.reg()], writes=[EI.reg()])
                    kb.op("dve", lambda en: en.scalar_tensor_tensor(out=QT2.t[:, 0:n], in0=pq.t[:, 0:n], scalar=HDK,
                                                                    in1=E_.t[:, 0:n], op0=ALU.mult, op1=ALU.mult),
                          reads=[pq.reg(), E_.reg()], writes=[QT2.reg()])
                    kb.op("dve", lambda en: en.tensor_tensor(out=KF.t[:, 0:n], in0=KK.t[:, 0:n], in1=EI.t[:, 0:n], op=ALU.mult),
                          reads=[KK.reg(), EI.reg()], writes=[KF.reg()])
                    kb.op("act", lambda en: en.activation(out=KT2.t[:, 0:n], in_=KF.t[:, 0:n], func=AF.Copy),
                          reads=[KF.reg()], writes=[KT2.reg()])
                    if stop == f"{L}r":
                        raise StopBuild
                    nb = n // 128
                    blocks = list(range(nb))[::-1] if bwd else list(range(nb))
                    OST = SQ
                    for bi in blocks:
                        o_ = bi * 128
                        blk = (t0 + o_) // 128
                        ptr = PS[3]
                        mm(ptr.t[:, 0:128], KT2.t[:, o_:o_ + 128], IDB.t[:, :], True, True, [KT2.reg(), IDB.reg()], [ptr.reg()])
                        for c in range(4):
                            eng = "dve"
                            if eng == "act":
                                kb.op("act", lambda en: en.activation(out=KM.t[:, c, :], in_=ptr.t[:, 0:128], func=AF.Identity,
                                                                      scale=HGSM.t[:, 16 + c:17 + c]),
                                      reads=[ptr.reg(), HGSM.reg()], writes=[KM.reg(c)])
                            else:
                                kb.op("dve", lambda en: en.tensor_scalar(out=KM.t[:, c, :], in0=ptr.t[:, 0:128],
                                                                         scalar1=HGSM.t[:, 16 + c:17 + c], scalar2=None,
                                                                         op0=ALU.mult),
                                      reads=[ptr.reg(), HGSM.reg()], writes=[KM.reg(c)])
                        if stop == f"{L}t1":
                            raise StopBuild
                        pa = PS[4]
                        mm(pa.t[:, 0:128], KT2.t[:, o_:o_ + 128], QT2.t[:, o_:o_ + 128], True, True,
                           [KT2.reg(), QT2.reg()], [pa.reg()])
                        am = AM[bi % 2]
                        kb.op("dve", lambda en: en.tensor_tensor(out=am.t[:, :], in0=pa.t[:, 0:128], in1=MASK[dirn], op=ALU.mult),
                              reads=[pa.reg(), MSK32.reg()], writes=[am.reg()])
                        if stop == f"{L}t2":
                            raise StopBuild
                        for c in range(4):
                            kb.op("dve", lambda en: en.tensor_tensor(
                                out=QM.t[:, c, :], in0=QT2.t[:, o_:o_ + 128], in1=CM2.t[:, c, :], op=ALU.mult),
                                reads=[QT2.reg(), CM2.reg()], writes=[QM.reg(c)])
                        po = PS[5]
                        mm(po.t[:, 0:128], VHD.t[:, blk, :], am.t[:, :], True, False, [VHD.reg(), am.reg()], [po.reg()])
                        if stop == f"{L}t3":
                            raise StopBuild
                        chunks = [3, 2, 1, 0] if bwd else [0, 1, 2, 3]
                        for ci, c in enumerate(chunks):
                            nst = cstate["n"]
                            slot = nst % 8
                            if ci == 0 and True:
                                pass
                            if not cstate.get("valid"):
                                kb.op("act", lambda en: en.activation(out=SBF.t[:, slot, :], in_=S, func=AF.Copy),
                                      reads=[SST.reg(h)], writes=[SBF.reg(slot)])
                                cstate["valid"] = True
                            mm(po.t[:, 0:128], SBF.t[:, slot, :], QM.t[:, c, :],
                               False, ci == 3, [SBF.reg(slot), QM.reg(c)], [po.reg()])
                            pd = PS[6 + ci % 2]
                            mm(pd.t[:, 0:128], KM.t[:, c, :], VHD.t[:, blk, :], True, True, [KM.reg(c), VHD.reg()], [pd.reg()])
                            gidx = o_ + 32 * c + (0 if bwd else 31)
                            gcol = E_.t[:, gidx:gidx + 1]
                            tt = STT_T[ci % 2]
                            kb.op("dve", lambda en: en.tensor_scalar(out=tt.t[:, :], in0=pd.t[:, 0:128], scalar1=gcol, scalar2=None,
                                                                     op0=ALU.mult), reads=[pd.reg(), E_.reg()], writes=[tt.reg()])
                            kb.op("dve", lambda en: en.scalar_tensor_tensor(out=S, in0=S, scalar=gcol, in1=tt.t[:, :],
                                                                            op0=ALU.mult, op1=ALU.add),
                                  reads=[SST.reg(h), E_.reg(), tt.reg()], writes=[SST.reg(h)])
                            cstate["n"] = nst + 1
                            nslot = (nst + 1) % 8
                            kb.op("act", lambda en: en.activation(out=SBF.t[:, nslot, :], in_=S, func=AF.Copy),
                                  reads=[SST.reg(h)], writes=[SBF.reg(nslot)])
                        kb.op("act", lambda en: en.activation(out=OST.t[:, 0, o_:o_ + 128], in_=po.t[:, 0:128], func=AF.Copy),
                              reads=[po.reg()], writes=[OST.reg(0)])
                        if stop == f"{L}t":
                            raise StopBuild
                    if not bwd:
                        kb.dma("sp", OADv[:, h, t0:t0 + n], OST.t[:, 0, 0:n], reads=[OST.reg(0)], writes=[EREG.reg("oad", h, ti)])
                    else:
                        OA = SQ
                        kb.dma("sp", OA.t[:, 1, 0:n], OADv[:, h, t0:t0 + n], reads=[EREG.reg("oad", h, ti)], writes=[OA.reg(1)])
                        kb.op("dve", lambda en: en.tensor_tensor(out=OST.t[:, 0, 0:n], in0=OST.t[:, 0, 0:n], in1=OA.t[:, 1, 0:n],
                                                                 op=ALU.add), reads=[OST.reg(0), OA.reg(1)], writes=[OST.reg(0)])
                        kb.op("act", lambda en: en.activation(out=SQ.t[:, 2, 0:n], in_=OST.t[:, 0, 0:n], func=AF.Square),
                              reads=[OST.reg(0)], writes=[SQ.reg(2)])
                        pss = PS[0]
                        mm(pss.t[:, 0:n], ONES128.t[:, :], SQ.t[:, 2, 0:n], True, True, [ONES128.reg(), SQ.reg(2)], [pss.reg()])
                        R0, R1 = TMP
                        kb.op("dve", lambda en: en.tensor_scalar(out=R0.t[:, 0:n], in0=pss.t[:, 0:n], scalar1=RMS_EPS, scalar2=None,
                                                                 op0=ALU.add), reads=[pss.reg()], writes=[R0.reg()])
                        kb.op("act", lambda en: en.activation(out=R0.t[:, 0:n], in_=R0.t[:, 0:n], func=AF.Sqrt),
                              reads=[R0.reg()], writes=[R0.reg()])
                        kb.op("dve", lambda en: en.reciprocal(out=R1.t[:, 0:n], in_=R0.t[:, 0:n]), reads=[R0.reg()], writes=[R1.reg()])
                        kb.op("dve", lambda en: en.scalar_tensor_tensor(out=OST.t[:, 0, 0:n], in0=OST.t[:, 0, 0:n],
                                                                        scalar=HGSM.t[:, 6 + e:7 + e], in1=R1.t[:, 0:n],
                                                                        op0=ALU.mult, op1=ALU.mult),
                              reads=[OST.reg(0), R1.reg(), HGSM.reg()], writes=[OST.reg(0)])
                        pg = PS[1]
                        for kc in range(NCH):
                            mm(pg.t[:, 0:n], WB[1].t[:, kc, h * 128:(h + 1) * 128], hT.t[:, kc, t0:t0 + n],
                               kc == 0, kc == NCH - 1, [WB[1].reg(), hT.reg(kc, ti)], [pg.reg()])
                        kb.op("act", lambda en: en.activation(out=R0.t[:, 0:n], in_=pg.t[:, 0:n], func=AF.Silu),
                              reads=[pg.reg()], writes=[R0.reg()])
                        stg = STG[h % 2]
                        kb.op("dve", lambda en: en.tensor_tensor(out=stg.t[:, 0:n], in0=OST.t[:, 0, 0:n], in1=R0.t[:, 0:n],
                                                                 op=ALU.mult), reads=[OST.reg(0), R0.reg()], writes=[stg.reg()])
                        kb.dma("sp", RECDv[:, h, t0:t0 + n], stg.t[:, 0:n], reads=[stg.reg()], writes=[EREG.reg("rec", h, ti)])
                    cstate["valid"] = False
            return

        scan_phase(0)
        if stop == f"{L}s":
            raise StopBuild
        kb.dma("sp", XS.ap(), SST.t[:, :, :].rearrange("p h d -> p (h d)"), reads=[SST.reg(h) for h in range(4)],
               writes=[EREG.reg("xs")])
        pairs = [[2 * i, 2 * i + 1] for i in range(N_CORES // 2)]
        kb.collective(lambda en: en.collective_compute("AllGather", ALU.bypass, replica_groups=pairs,
                                                       ins=[XS.ap().opt()], outs=[XG.ap().opt()]),
                      reads=[EREG.reg("xs")], writes=[EREG.reg("xg")])
        kb.collective(lambda en: en.collective_compute("AllGather", ALU.bypass, replica_groups=pairs,
                                                       ins=[HS.ap().opt()], outs=[HG.ap().opt()]),
                      reads=[EREG.reg("hs")], writes=[EREG.reg("hg")])
        for r_ in range(2):
            kb.dma("sp", TMP[r_].t[:, :], XG.ap()[r_ * 128:(r_ + 1) * 128, :], reads=[EREG.reg("xg")], writes=[TMP[r_].reg()])
            kb.dma("sp", STG[r_].t[0:8, :], HG.ap()[r_ * 128:r_ * 128 + 8, :], reads=[EREG.reg("hg")], writes=[STG[r_].reg()])
        kb.op("dve", lambda en: en.tensor_scalar(out=SRX.t[:, :], in0=TMP[0].t[:, :], scalar1=HGSM.t[:, 4:5], scalar2=None,
                                                 op0=ALU.mult), reads=[TMP[0].reg(), HGSM.reg()], writes=[SRX.reg()])
        kb.op("dve", lambda en: en.scalar_tensor_tensor(out=SRX.t[:, :], in0=TMP[1].t[:, :], scalar=HGSM.t[:, 5:6],
                                                        in1=SRX.t[:, :], op0=ALU.mult, op1=ALU.add),
              reads=[TMP[1].reg(), HGSM.reg(), SRX.reg()], writes=[SRX.reg()])
        for r_ in range(2):
            kb.op("act", lambda en: en.activation(out=SIL[r_].t[0:8, :], in_=STG[r_].t[0:8, :], func=AF.Copy),
                  reads=[STG[r_].reg()], writes=[SIL[r_].reg()])
        kb.op("dve", lambda en: en.tensor_scalar(out=SQ.t[0:8, 0, :], in0=SIL[0].t[0:8, :], scalar1=HGSM.t[0:8, 4:5], scalar2=None,
                                                 op0=ALU.mult), reads=[SIL[0].reg(), HGSM.reg()], writes=[SQ.reg(0)])
        kb.op("dve", lambda en: en.scalar_tensor_tensor(out=UHB.t[0:8, :], in0=SIL[1].t[0:8, :], scalar=HGSM.t[0:8, 5:6],
                                                        in1=SQ.t[0:8, 0, :], op0=ALU.mult, op1=ALU.add),
              reads=[SIL[1].reg(), HGSM.reg(), SQ.reg(0), UHB.reg()], writes=[UHB.reg()])
        pool_block(15, [usrc(14) + (0,), usrc(15) + (1,), (UHB.reg(), lambda g: UHB.t[:, g * 128:(g + 1) * 128], 5)])
        scan_phase(1)

        kb.alias([XIN], [VHD, QT2, KT2, KM, SBF] + AM)
        for i in range(2):
            kb.dma("pool", WA[i].t[:, :, :], wv(L, "mout", 0, NCH, i * 512, 512), reads=[WREG[L]], writes=[WA[i].reg()])
        for ti in tiles:
            t0, n = TILES[ti]
            s_ = sidx(ti)
            kb.dma("sp", XIN.t[:, 0:4, 0:n], RECDv[:, :, t0:t0 + n], reads=[EREG.reg("rec", h, ti) for h in range(4)],
                   writes=[XIN.reg(0)])
            kb.dma("sp", XIN.t[:, 4:8, 0:n], YPDv[:, :, t0:t0 + n],
                   reads=[EREG.reg("ypd", tb) for tb in range(t0 // 128, (t0 + n) // 128)], writes=[XIN.reg(1)])
            for dc in range(NCH):
                py = PS[4 + dc % 4]
                wbuf, cc = WA[dc // 4], dc % 4
                for kc in range(NCH):
                    mm(py.t[:, 0:n], wbuf.t[:, kc, cc * 128:(cc + 1) * 128], XIN.t[:, kc, 0:n],
                       kc == 0, kc == NCH - 1, [wbuf.reg(), XIN.reg(kc // 4)], [py.reg()])
                kb.op("dve", lambda en: en.scalar_tensor_tensor(
                    out=xT.t[:, dc, t0:t0 + n], in0=py.t[:, 0:n], scalar=DER.t[:, L, 7, dc, s_:s_ + 1],
                    in1=xT.t[:, dc, t0:t0 + n], op0=ALU.mult, op1=ALU.add),
                    reads=[py.reg(), DER.reg(L, 'a'), xT.reg(dc, ti)], writes=[xT.reg(dc, ti)])
        kb.alias(FFN_SCR, HG_SCR)

    def finish():
        evs = []
        for ti, (t0, n) in enumerate(TILES[:4]):
            evs.append(kb.dma("sp", out_d[:, :, t0:t0 + n], xT.t[:, :, t0:t0 + n],
                              reads=[xT.reg(c, ti) for c in range(NCH)]))
        evs.append(kb.dma("sp", dbg_d[:, :, :], xT.t[:, :, TL:TT], reads=[xT.reg(c, 4) for c in range(NCH)]))
        for s, v in evs:
            kb._wait("sp", s, v)
        for i, sem in enumerate(kb.dsems):
            if kb.dval[i]:
                kb._wait("sp", sem, kb.dval[i])
        es.close()
        return nc

    ALLT = [0, 1, 2, 3, 4]
    for li, L in enumerate(layers):
        last = L == DEPTH - 1
        Ln = layers[li + 1] if li + 1 < len(layers) else None
        ffn(L, 0, ALLT)
        ln_pass(L, 0, ALLT)
        if stop == f"{L}a":
            return finish()
        mtiles = [0, 1, 2, 3] if last else ALLT
        try:
            if L % 2 == 1:
                attn_mixer(L, mtiles)
            else:
                hgrn_mixer(L, mtiles)
        except StopBuild:
            return finish()
        ln_pass(L, 1, mtiles)
        if stop == f"{L}b":
            return finish()
        ffn(L, 1, mtiles)
        if Ln is not None:
            compute_mods(Ln)
            compute_der(Ln, "a")
            compute_der(L, "b")
        ln_pass(L, 2, mtiles, want_h=(Ln is not None))
        if stop == f"{L}c":
            return finish()
    return finish()


_CACHE = {}


def _get_nc(stop, layers):
    key = (stop, tuple(layers))
    if key not in _CACHE:
        _CACHE[key] = build_program(stop, tuple(layers))
    return _CACHE[key]


def pack_weights(inputs, L):
    off, rows = wlayout(L)
    flat = np.zeros(8 * rows * 2048, np.float32)
    i = L // 2
    src = {"ada": inputs["w_ada"][L], "fin0": inputs["w_ffn_in"][L, 0], "fin1": inputs["w_ffn_in"][L, 1],
           "fout0": inputs["w_ffn_out"][L, 0], "fout1": inputs["w_ffn_out"][L, 1]}
    if L % 2 == 0:
        src.update(min=inputs["w_in_even"][i], mout=inputs["w_out_even"][i], poolw=inputs["pool_w"][i])
    else:
        src.update(min=inputs["w_in_odd"][i], mout=inputs["w_out_odd"][i])
    for name, (o, r, c) in off.items():
        flat[o:o + r * c] = np.asarray(src[name], np.float32).reshape(-1)
    return flat.reshape(8, rows, 2048)


def make_in_maps(inputs, layers=tuple(range(DEPTH))):
    x = np.asarray(inputs["x"], np.float32)
    ctx = np.asarray(inputs["ctx"], np.float32)
    c = np.asarray(inputs["c"], np.float32)
    c_ctx = np.asarray(inputs["c_ctx"], np.float32)
    b_adaT = np.ascontiguousarray(np.asarray(inputs["b_ada"], np.float32).reshape(DEPTH, 72, 128).transpose(2, 0, 1))
    lng = np.asarray(inputs["ln_g"], np.float32).reshape(DEPTH, 3, NCH, 128)
    lnb = np.asarray(inputs["ln_b"], np.float32).reshape(DEPTH, 3, NCH, 128)
    lnT = np.ascontiguousarray(np.stack([lng, lnb], axis=2).transpose(4, 0, 1, 2, 3))
    shared = {"b_adaT": b_adaT, "lnT": lnT}
    has_even = any(L % 2 == 0 for L in layers)
    if has_even:
        sidx_ = np.arange(128)
        same = (sidx_[:, None] // 32) == (sidx_[None, :] // 32)
        maskF = (same & (sidx_[:, None] <= sidx_[None, :])).astype(np.float32)
        maskB = (same & (sidx_[:, None] >= sidx_[None, :])).astype(np.float32)
        rst = np.broadcast_to((np.arange(512) % 32 != 0).astype(np.float32), (128, 512))
        shared["hgc"] = np.ascontiguousarray(np.concatenate([maskF, maskB, rst], axis=1))
        shared["idf"] = np.eye(128, dtype=np.float32)
        cm2 = np.zeros((128, 4, 128), np.float32)
        for c_ in range(4):
            cm2[:, c_, 32 * c_:32 * c_ + 32] = 1.0
        shared["cm2"] = cm2.reshape(128, 512)
        shared["hglbT"] = np.ascontiguousarray(
            np.asarray(inputs["hg_lb"], np.float32).reshape(2, 2, 4, 128).transpose(3, 0, 1, 2))
        nw = np.asarray(inputs["hg_norm_w"], np.float32)
        psc = np.asarray(inputs["pool_scale"], np.float32).reshape(2, 4, 128)

        def bands(flip):
            out = np.zeros((128, 6, 4, 128), np.float32)
            for g, w in enumerate((2, 4, 8, 16)):
                lo, hi = (-(w // 2 - 1), w // 2) if flip else (-(w // 2), w // 2 - 1)
                for t in range(128):
                    for d in range(lo, hi + 1):
                        sp = t + d
                        cnt_first = sum(1 for dd in range(lo, hi + 1) if t + dd >= 0)
                        cnt_last = sum(1 for dd in range(lo, hi + 1) if t + dd <= 127)
                        self_ = 1.0 if d == 0 else 0.0
                        if sp < 0:
                            out[sp + 128, 0, g, t] = 1.0 / w
                        elif sp > 127:
                            out[sp - 128, 2, g, t] = 1.0 / w
                            i = 7 - (sp - 128)
                            out[i, 5, g, t] = 1.0 / w
                        else:
                            out[sp, 1, g, t] = 1.0 / w - self_
                        if 0 <= sp <= 127:
                            out[sp, 3, g, t] = 1.0 / cnt_first - self_
                            out[sp, 4, g, t] = 1.0 / cnt_last - self_
            return np.ascontiguousarray(out.reshape(128, -1))
        band_np = [bands(False), bands(True)]
    has_odd = any(L % 2 == 1 for L in layers)
    if has_odd:
        rot = np.zeros((128, 128), np.float32)
        for base in (0, 64):
            for blk in (0, 32):
                for j in range(16):
                    rot[base + blk + j + 16, base + blk + j] = -1.0
                    rot[base + blk + j, base + blk + j + 16] = 1.0
        shared["rotm"] = rot
        shared["dalam"] = np.ascontiguousarray(np.broadcast_to(
            np.asarray(inputs["da_lambda"], np.float32).reshape(1, 2, 256), (128, 2, 256)))
        shared["subwT"] = np.ascontiguousarray(np.asarray(inputs["da_sub_w"], np.float32).T)
        inv_freq = (10000.0 ** (-np.arange(0, 32, 2, dtype=np.float32) / 32.0)).astype(np.float32)
    packed = {L: pack_weights(inputs, L) for L in layers}
    maps = []
    for r in range(N_CORES):
        b, half = r // 2, r % 2
        xs = x[b, half * TL:(half + 1) * TL]
        cs = ctx[b]
        if half == 1:
            xs = xs[::-1]
            cs = cs[::-1]
        m = dict(shared)
        for L in layers:
            m[f"wsh{L}"] = packed[L][r]
        m["xT"] = np.ascontiguousarray(xs.reshape(TL, NCH, 128).transpose(2, 1, 0))
        m["ctxT"] = np.ascontiguousarray(cs.reshape(TC, NCH, 128).transpose(2, 1, 0))
        if has_even:
            m["band"] = band_np[half]
            sm = np.zeros((128, 20), np.float32)
            sm[:, 0:4] = (1, 0, 0, 1) if half == 0 else (0, 1, 1, 0)
            sm[:, 4:6] = (0, 1) if half == 0 else (1, 0)
            sm[:, 6:8] = nw.T
            sm[:, 8:12] = psc[0].T
            sm[:, 12:16] = psc[1].T
            for c_ in range(4):
                sm[32 * c_:32 * c_ + 32, 16 + c_] = 1.0
            m["hgsm"] = sm
        if has_odd:
            pos = np.arange(half * TL, (half + 1) * TL)
            if half == 1:
                pos = pos[::-1]
            row = (pos // 64).astype(np.float32)
            col = (pos % 64).astype(np.float32)
            ang_r = row[:, None] * inv_freq
            ang_c = col[:, None] * inv_freq
            ang = np.concatenate([ang_r, ang_r, ang_c, ang_c], axis=-1).astype(np.float32)
            ang = np.concatenate([ang, ang], axis=-1).T
            m["ropeT"] = np.ascontiguousarray(np.stack([np.cos(ang), np.sin(ang)], axis=1).astype(np.float32))
        cc = np.stack([c[b], c_ctx], axis=-1)
        m["cT"] = np.ascontiguousarray(cc.reshape(NCH, 128, 2).transpose(1, 0, 2))
        maps.append(m)
    return maps


def gather_out(results):
    out = np.empty((4, 2 * TL, D), np.float32)
    ctxs = []
    for r in range(N_CORES):
        b, half = r // 2, r % 2
        o = results[r]["outT"].transpose(2, 1, 0).reshape(TL, D)
        if half == 1:
            o = o[::-1]
        out[b, half * TL:(half + 1) * TL] = o
        cdbg = results[r]["dbgT"].transpose(2, 1, 0).reshape(TC, D)
        ctxs.append(cdbg[::-1] if half == 1 else cdbg)
    return out, ctxs


def run(inputs, stop="full", layers=tuple(range(DEPTH))):
    nc = _get_nc(stop, layers)
    res = run_bass_kernel_spmd(nc, make_in_maps(inputs, layers), core_ids=list(range(N_CORES)))
    return gather_out(res.results)


def kernel(**inputs):
    out, _ = run(inputs)
    return out
```
